# Optimizing a Trainium2 kernel written in Bass

```python
import math
import jax, jax.numpy as jnp
from jax import lax
import numpy as np

D_MODEL = 1024
BATCH = 8
SEQ = 2048
DEPTH = 4
DEC_BATCH = 16
DEC_SEQ = 2048
PAST_LEN = 128

N_MIXERS = 2
N_ATTN_LAYERS = (DEPTH + 1) // 2
N_CONV_LAYERS = DEPTH // 2
ATTN_HEAD_DIM = 128
ATTN_HEADS = D_MODEL // ATTN_HEAD_DIM
WINDOWS = (128, 512, 2048)
DILATIONS = (1, 4, 16)
N_GROUPS = len(WINDOWS)
QKV_COLS = N_GROUPS * 3 * ATTN_HEADS * ATTN_HEAD_DIM
ROPE_THETA = 500000.0
ROT_DIM = ATTN_HEAD_DIM // 4
NEG_INF = -1e30
CONV_WIDTH = 3
PEER_HEADS = 8
PEER_N_KEYS = 128
PEER_N_EXPERTS = PEER_N_KEYS * PEER_N_KEYS
PEER_QUERY_DIM = 256
PEER_HALF = PEER_QUERY_DIM // 2
PEER_TOPK = 16
PEER_CHUNK = 128
PLE_DIM = 256
DEEPNORM_ALPHA = (2.0 * DEPTH) ** 0.25
DEEPNORM_BETA = (8.0 * DEPTH) ** -0.25
LN_EPS = 1e-5

kernel_name = "hybrid_dilated_attn_shortconv_peer_encoder"


def layer_norm(x, g, b):
    xf = x.astype(jnp.float32)
    mu = jnp.mean(xf, axis=-1, keepdims=True)
    xc = xf - mu
    var = jnp.mean(xc * xc, axis=-1, keepdims=True)
    return (xc * lax.rsqrt(var + LN_EPS) * g.astype(jnp.float32) + b.astype(jnp.float32)).astype(x.dtype)


def rope_tables(seq_len):
    inv_freq = ROPE_THETA ** (-jnp.arange(0, ROT_DIM, 2, dtype=jnp.float32) / ROT_DIM)
    ang = jnp.arange(seq_len, dtype=jnp.float32)[:, None] * inv_freq[None, :]
    return jnp.cos(ang), jnp.sin(ang)


def apply_partial_rope(t, cos, sin):
    rot = t[..., :ROT_DIM].astype(jnp.float32)
    x1, x2 = rot[..., :ROT_DIM // 2], rot[..., ROT_DIM // 2:]
    c, s = cos[None, :, None, :], sin[None, :, None, :]
    rotated = jnp.concatenate([x1 * c - x2 * s, x2 * c + x1 * s], axis=-1).astype(t.dtype)
    return jnp.concatenate([rotated, t[..., ROT_DIM:]], axis=-1)


def dilated_window_attention(q, k, v, dil, steps):
    B, S, H, Dh = q.shape
    L = S // dil
    Lp = -(-L // steps) * steps
    nb = Lp // steps
    pad = Lp - L
    Bd = B * dil

    def to_res(t):
        return t.reshape(B, L, dil, H, Dh).transpose(0, 2, 1, 3, 4).reshape(Bd, L, H, Dh)

    qb = jnp.pad(to_res(q), ((0, 0), (0, pad), (0, 0), (0, 0))).reshape(Bd, nb, steps, H, Dh)

    def windows(t):
        tp = jnp.pad(to_res(t), ((0, 0), (steps, steps + pad), (0, 0), (0, 0)))
        tp = tp.reshape(Bd, nb + 2, steps, H, Dh)
        return jnp.concatenate([tp[:, :-2], tp[:, 1:-1], tp[:, 2:]], axis=2)

    kw, vw = windows(k), windows(v)
    qpos = jnp.arange(Lp).reshape(nb, steps)
    kpos = jnp.arange(nb)[:, None] * steps - steps + jnp.arange(3 * steps)[None, :]
    dq = qpos[:, :, None] - kpos[:, None, :]
    valid = (jnp.abs(dq) <= steps) & (kpos[:, None, :] >= 0) & (kpos[:, None, :] < L)

    scale = 1.0 / math.sqrt(Dh)
    s = jnp.einsum('xnqhd,xnkhd->xnhqk', qb, kw).astype(jnp.float32) * scale
    s = jnp.where(valid[None, :, None], s, NEG_INF)
    m = jnp.max(s, axis=-1, keepdims=True)
    p = jnp.exp(s - m)
    den = jnp.sum(p, axis=-1)
    o = jnp.einsum('xnhqk,xnkhd->xnqhd', p, vw.astype(jnp.float32))
    o = o / jnp.transpose(den, (0, 1, 3, 2))[..., None]
    lse = m[..., 0] + jnp.log(den)

    o = o.reshape(Bd, Lp, H, Dh)[:, :L]
    o = o.reshape(B, dil, L, H, Dh).transpose(0, 2, 1, 3, 4).reshape(B, S, H, Dh)
    lse = jnp.transpose(lse, (0, 1, 3, 2)).reshape(Bd, Lp, H)[:, :L]
    lse = lse.reshape(B, dil, L, H).transpose(0, 2, 1, 3).reshape(B, S, H)
    return o, lse


def dilated_mixture_attention(x, w_qkv, w_out):
    B, S, _ = x.shape
    qkv = (x @ w_qkv).reshape(B, S, N_GROUPS, 3, ATTN_HEADS, ATTN_HEAD_DIM)
    cos, sin = rope_tables(S)
    outs, lses = [], []
    for g in range(N_GROUPS):
        dil = DILATIONS[g]
        steps = WINDOWS[g] // (2 * dil)
        q = apply_partial_rope(qkv[:, :, g, 0], cos, sin)
        k = apply_partial_rope(qkv[:, :, g, 1], cos, sin)
        o, lse = dilated_window_attention(q, k, qkv[:, :, g, 2], dil, steps)
        outs.append(o)
        lses.append(lse)
    wgt = jax.nn.softmax(jnp.stack(lses, axis=0), axis=0)
    o = jnp.sum(wgt[..., None] * jnp.stack(outs, axis=0), axis=0).astype(x.dtype)
    return o.reshape(B, S, ATTN_HEADS * ATTN_HEAD_DIM) @ w_out


def short_conv_mixer(x, w_in, conv_kernel, w_out):
    gb, gc, xv = jnp.split(x @ w_in, 3, axis=-1)
    h = gc * xv
    hp = jnp.pad(h, ((0, 0), (1, 1), (0, 0)))
    conv = hp[:, :-2] * conv_kernel[0] + hp[:, 1:-1] * conv_kernel[1] + hp[:, 2:] * conv_kernel[2]
    return (gb * conv) @ w_out


def peer_channel_mixer(x, w_query, sub_keys, u_tab, v_tab):
    B, S, D = x.shape
    xt = x.reshape(-1, PEER_CHUNK, D)

    def block(xc):
        C = xc.shape[0]
        q = (xc @ w_query).reshape(C, PEER_HEADS, 2, PEER_HALF)
        s = jnp.einsum('chpk,hpnk->chpn', q, sub_keys).astype(jnp.float32)
        sv, si = lax.top_k(s, PEER_TOPK)
        cand = (sv[:, :, 0, :, None] + sv[:, :, 1, None, :]).reshape(C, PEER_HEADS, PEER_TOPK * PEER_TOPK)
        cidx = (si[:, :, 0, :, None] * PEER_N_KEYS + si[:, :, 1, None, :]).reshape(C, PEER_HEADS, PEER_TOPK * PEER_TOPK)
        fv, fi = lax.top_k(cand, PEER_TOPK)
        eidx = jnp.take_along_axis(cidx, fi, axis=-1)
        g = jax.nn.softmax(fv, axis=-1)
        u = u_tab[eidx]
        h = jax.nn.gelu(jnp.einsum('chkd,cd->chk', u, xc).astype(jnp.float32), approximate=False)
        coef = (g * h).astype(xc.dtype)
        return jnp.einsum('chk,chkd->cd', coef, v_tab[eidx])

    return lax.map(block, xt).reshape(B, S, D)


def trunk(x, p, attn_w_qkv, attn_w_out, conv_w_in, conv_kernel, conv_w_out,
          peer_w_query, peer_sub_keys, peer_u, peer_v,
          ln1_gain, ln1_bias, ln2_gain, ln2_bias, ple_w_proj, ple_w_gate):
    for i in range(DEPTH):
        j = i // N_MIXERS
        if i % N_MIXERS == 0:
            h = dilated_mixture_attention(x, attn_w_qkv[j], attn_w_out[j])
        else:
            h = short_conv_mixer(x, conv_w_in[j], conv_kernel[j], conv_w_out[j])
        x = layer_norm(DEEPNORM_ALPHA * x + h, ln1_gain[i], ln1_bias[i])
        f = peer_channel_mixer(x, peer_w_query[i], peer_sub_keys[i], peer_u[i], peer_v[i])
        x = layer_norm(DEEPNORM_ALPHA * x + f, ln2_gain[i], ln2_bias[i])
        x = x + jax.nn.sigmoid(x @ ple_w_gate[i]) * (p[i] @ ple_w_proj[i])
    return x


def setup_inputs(seed: int = 0) -> dict:
    key = jax.random.key(seed)
    ks = jax.random.split(key, 20)
    f32 = jnp.float32
    D = D_MODEL
    nrm = lambda k, shape, scale: jax.random.normal(k, shape, f32) * scale
    return {
        "x_prompt": nrm(ks[0], (BATCH, SEQ, D), 1.0),
        "x_sample": nrm(ks[1], (DEC_BATCH, DEC_SEQ, D), 1.0),
        "p_prompt": nrm(ks[2], (DEPTH, BATCH, SEQ, PLE_DIM), 1.0),
        "p_sample": nrm(ks[3], (DEPTH, DEC_BATCH, DEC_SEQ, PLE_DIM), 1.0),
        "attn_w_qkv": nrm(ks[4], (N_ATTN_LAYERS, D, QKV_COLS), D ** -0.5),
        "attn_w_out": nrm(ks[5], (N_ATTN_LAYERS, ATTN_HEADS * ATTN_HEAD_DIM, D), DEEPNORM_BETA * (ATTN_HEADS * ATTN_HEAD_DIM) ** -0.5),
        "conv_w_in": nrm(ks[6], (N_CONV_LAYERS, D, 3 * D), D ** -0.5),
        "conv_kernel": nrm(ks[7], (N_CONV_LAYERS, CONV_WIDTH, D), CONV_WIDTH ** -0.5),
        "conv_w_out": nrm(ks[8], (N_CONV_LAYERS, D, D), DEEPNORM_BETA * D ** -0.5),
        "peer_w_query": nrm(ks[9], (DEPTH, D, PEER_HEADS * PEER_QUERY_DIM), D ** -0.5),
        "peer_sub_keys": nrm(ks[10], (DEPTH, PEER_HEADS, 2, PEER_N_KEYS, PEER_HALF), PEER_HALF ** -0.5),
        "peer_u": nrm(ks[11], (DEPTH, PEER_N_EXPERTS, D), D ** -0.5),
        "peer_v": nrm(ks[12], (DEPTH, PEER_N_EXPERTS, D), DEEPNORM_BETA * PEER_HEADS ** -0.5),
        "ln1_gain": 1.0 + nrm(ks[13], (DEPTH, D), 0.01),
        "ln1_bias": nrm(ks[14], (DEPTH, D), 0.01),
        "ln2_gain": 1.0 + nrm(ks[15], (DEPTH, D), 0.01),
        "ln2_bias": nrm(ks[16], (DEPTH, D), 0.01),
        "ple_w_proj": nrm(ks[17], (DEPTH, PLE_DIM, D), PLE_DIM ** -0.5),
        "ple_w_gate": nrm(ks[18], (DEPTH, D, D), D ** -0.5),
    }


def reference(x_prompt, x_sample, p_prompt, p_sample, attn_w_qkv, attn_w_out, conv_w_in,
              conv_kernel, conv_w_out, peer_w_query, peer_sub_keys, peer_u, peer_v,
              ln1_gain, ln1_bias, ln2_gain, ln2_bias, ple_w_proj, ple_w_gate):
    y_prompt = trunk(x_prompt, p_prompt, attn_w_qkv, attn_w_out, conv_w_in, conv_kernel, conv_w_out,
                     peer_w_query, peer_sub_keys, peer_u, peer_v,
                     ln1_gain, ln1_bias, ln2_gain, ln2_bias, ple_w_proj, ple_w_gate)
    y_sample = trunk(x_sample, p_sample, attn_w_qkv, attn_w_out, conv_w_in, conv_kernel, conv_w_out,
                     peer_w_query, peer_sub_keys, peer_u, peer_v,
                     ln1_gain, ln1_bias, ln2_gain, ln2_bias, ple_w_proj, ple_w_gate)
    return (y_prompt, y_sample)
```

```python
import math
import numpy as np
import concourse.bass as bass
import concourse.mybir as mybir
from concourse.bass_utils import run_bass_kernel_spmd

ALU = mybir.AluOpType
AF = mybir.ActivationFunctionType
AX = mybir.AxisListType
F32 = mybir.dt.float32
BF16 = mybir.dt.bfloat16
U32 = mybir.dt.uint32

D = 1024
S_LEN = 2048
DEPTH = 4
NT = 16
ALPHA = (2.0 * DEPTH) ** 0.25
LN_EPS = 1e-5
DILS = (1, 4, 16)
NEG = -1e30


class _Op:
    __slots__ = ("eng", "emit", "deps", "needs_inc", "seq", "sem", "semval", "is_dma")


class _Res:
    __slots__ = ("w", "r", "rd")

    def __init__(self):
        self.w = None
        self.r = {}
        self.rd = []


class Sched:
    ENGS = ("pe", "act", "dve", "pool", "sp")

    def __init__(self, nc):
        self.nc = nc
        self.ops = {e: [] for e in self.ENGS}
        self.res = {}
        self.esem = {e: nc.alloc_semaphore(name="s_" + e) for e in self.ENGS}
        self.dsem = {}
        self.nops = 0
        self.phase_ops = []
        self.phase_bar = None

    def _deps(self, op, reads, writes):
        deps = []
        if self.phase_bar is not None:
            deps.append(self.phase_bar)
        for k in reads:
            r = self.res.get(k)
            if r is None:
                r = self.res[k] = _Res()
            if r.w is not None:
                deps.append(r.w)
        for k in writes:
            r = self.res.get(k)
            if r is None:
                r = self.res[k] = _Res()
            if r.w is not None:
                deps.append(r.w)
            deps.extend(r.r.values())
            deps.extend(r.rd)
        for k in reads:
            if op.is_dma:
                self.res[k].rd.append(op)
            else:
                self.res[k].r[op.eng] = op
        for k in writes:
            r = self.res[k]
            r.w = op
            r.r = {}
            r.rd = []
        self._set(op, deps)
        self.phase_ops.append(op)

    def _set(self, op, deps):
        out = []
        seen = set()
        for d in deps:
            if d is op or id(d) in seen:
                continue
            seen.add(id(d))
            if d.eng == "pe" and op.eng == "pe" and not d.is_dma and not op.is_dma:
                continue
            if not d.is_dma:
                d.needs_inc = True
            out.append(d)
        op.deps = out

    def _new(self, eng, emit, is_dma):
        o = _Op()
        o.eng = eng
        o.emit = emit
        o.needs_inc = False
        o.is_dma = is_dma
        o.seq = 0
        o.sem = None
        o.semval = 0
        return o

    def op(self, eng, emit, reads=(), writes=()):
        o = self._new(eng, emit, False)
        self._deps(o, reads, writes)
        self.ops[eng].append(o)
        self.nops += 1
        return o

    def dma(self, q, emit, reads=(), writes=(), key=None):
        o = self._new(q, emit, True)
        if key is None:
            key = writes[0] if writes else reads[0]
        ds = self.dsem.get(key)
        if ds is None:
            ds = self.dsem[key] = [self.nc.alloc_semaphore(name="d_%d" % len(self.dsem)), 0]
        ds[1] += 16
        o.sem = ds[0]
        o.semval = ds[1]
        self._deps(o, reads, writes)
        self.ops[q].append(o)
        self.nops += 1
        return o

    def barrier(self, emit):
        o = self._new("pool", emit, False)
        last = {}
        for p in self.phase_ops:
            k = ("dma", id(p.sem)) if p.is_dma else ("eng", p.eng)
            last[k] = p
        deps = list(last.values())
        self._set(o, deps)
        self.ops["pool"].append(o)
        self.phase_ops = [o]
        self.phase_bar = o
        self.res = {}

    def emit_all(self):
        nc = self.nc
        for e in self.ENGS:
            c = 0
            for o in self.ops[e]:
                if o.is_dma:
                    continue
                if o.needs_inc:
                    c += 1
                    o.seq = c
        finals = [(ds[0], ds[1]) for ds in self.dsem.values()]
        sched = self

        def run(e, eng):
            waited = {}
            for o in sched.ops[e]:
                need = {}
                for d in o.deps:
                    if d.is_dma:
                        s, v = d.sem, d.semval
                    else:
                        s, v = sched.esem[d.eng], d.seq
                    k = id(s)
                    if waited.get(k, 0) >= v:
                        continue
                    cur = need.get(k)
                    if cur is None or cur[1] < v:
                        need[k] = (s, v)
                for k, (s, v) in need.items():
                    waited[k] = v
                    eng.wait_ge(s, v)
                ins = o.emit(eng)
                if o.is_dma:
                    ins.then_inc(o.sem, 16)
                elif o.needs_inc:
                    ins.then_inc(sched.esem[e], 1)
            if e == "sp":
                for s, v in finals:
                    eng.wait_ge(s, v)

        with nc.Block() as block:
            @block.sync
            def _(eng):
                run("sp", eng)

            @block.tensor
            def _(eng):
                run("pe", eng)

            @block.scalar
            def _(eng):
                run("act", eng)

            @block.vector
            def _(eng):
                run("dve", eng)

            @block.gpsimd
            def _(eng):
                run("pool", eng)


def _dsize(dt):
    return 2 if dt == BF16 else 4


class KB:
    ARENA_BYTES = 140 * 1024

    def __init__(self, nc):
        self.nc = nc
        self.S = Sched(nc)
        A = nc.alloc_sbuf_tensor
        self.X = A("X", [128, NT, D], F32)
        self.arena = A("arena", [128, self.ARENA_BYTES // 4], F32)
        self.aoff = 0
        self.dummy = A("kdummy", [128, 8], F32)
        self.ident = A("kident", [128, 128], BF16)
        self.ones = A("kones", [128, 128], BF16)
        self.iota_n = A("iota_n", [128, 128], F32)
        self.mask = A("kmask", [128, 3, 128], BF16)
        self.cs_d = nc.dram_tensor("cs_scr", [128, 3 * 2 * NT * 16], F32).ap()
        self.eps_t = A("eps_t", [128, 1], F32)
        self.ps = [nc.alloc_psum_tensor("ps%d" % i, [128, 512], F32)[:] for i in range(6)]
        ps67 = nc.alloc_psum_tensor("ps67", [128, 1024], F32)
        self.ps_y = ps67[:]
        self.ps.append(ps67[:, 0:512])
        self.ps.append(ps67[:, 512:1024])
        self._uid = 0

    def uid(self, base):
        self._uid += 1
        return "%s#%d" % (base, self._uid)

    def carve_reset(self):
        self.aoff = 0

    def carve(self, shape, dt):
        n = 1
        for s in shape[1:]:
            n *= s
        nbytes = (n * _dsize(dt) + 31) // 32 * 32
        assert self.aoff + nbytes <= self.ARENA_BYTES, ("arena overflow", self.aoff, nbytes)
        ap = self.arena[:, self.aoff // 4:(self.aoff + nbytes) // 4]
        self.aoff += nbytes
        if dt != F32:
            ap = ap.bitcast(dt)
        ap = ap[:, 0:n]
        if len(shape) > 2:
            names = " ".join("a%d" % i for i in range(len(shape) - 1))
            kw = {"a%d" % i: shape[i + 1] for i in range(1, len(shape) - 1)}
            ap = ap.rearrange("p (%s) -> p %s" % (names, names), **kw)
        if shape[0] != 128:
            ap = ap[0:shape[0]]
        return ap

    def dma(self, out, in_, r=(), w=(), key=None, q="sp"):
        return self.S.dma(q, lambda e: e.dma_start(out=out, in_=in_), r, w, key)

    def mm(self, out, lhsT, rhs, start, stop, r, w):
        return self.S.op("pe", lambda e: e.matmul(out, lhsT=lhsT, rhs=rhs, start=start, stop=stop), r, w)

    def tr(self, out, in_, r, w):
        ident = self.ident
        n = in_.shape[0]
        return self.S.op("pe", lambda e: e.transpose(out=out, in_=in_, identity=ident[0:n, 0:n]), r, w)

    def act(self, out, in_, func, r, w, bias=None, scale=None, accum=None):
        kw = {}
        if bias is not None:
            kw["bias"] = bias
        if scale is not None:
            kw["scale"] = scale
        if accum is not None:
            kw["accum_out"] = accum
        return self.S.op("act", lambda e: e.activation(out=out, in_=in_, func=func, **kw), r, w)

    def cp(self, eng, out, in_, r, w):
        if eng == "act":
            return self.S.op("act", lambda e: e.copy(out=out, in_=in_), r, w)
        return self.S.op(eng, lambda e: e.tensor_copy(out=out, in_=in_), r, w)

    def tt(self, eng, out, in0, in1, op, r, w):
        return self.S.op(eng, lambda e: e.tensor_tensor(out=out, in0=in0, in1=in1, op=op), r, w)

    def ts(self, eng, out, in0, s1, s2, op0, op1, r, w):
        if s2 is None:
            return self.S.op(eng, lambda e: e.tensor_scalar(out=out, in0=in0, scalar1=s1, scalar2=None, op0=op0), r, w)
        return self.S.op(eng, lambda e: e.tensor_scalar(out=out, in0=in0, scalar1=s1, scalar2=s2, op0=op0, op1=op1), r, w)

    def stt(self, eng, out, in0, scalar, in1, op0, op1, r, w):
        return self.S.op(eng, lambda e: e.scalar_tensor_tensor(out=out, in0=in0, scalar=scalar, in1=in1, op0=op0, op1=op1), r, w)

    def barrier(self, keep=0):
        dummy = self.dummy
        self.S.barrier(lambda e: e.memset(dummy[:, 0:8], 0.0))
        self.aoff = keep

    def build_consts(self):
        nc = self.nc
        S = self.S
        self.carve_reset()
        dfi = self.carve([128, 128], F32)
        iota_n, ident, ones, mask = self.iota_n, self.ident, self.ones, self.mask
        S.op("pool", lambda e: e.iota(iota_n[:], pattern=[[1, 128]], base=0, channel_multiplier=0,
                                      allow_small_or_imprecise_dtypes=True), (), ["iota_n"])
        S.op("pool", lambda e: e.iota(dfi, pattern=[[1, 128]], base=0, channel_multiplier=-1,
                                      allow_small_or_imprecise_dtypes=True), (), ["dfi"])
        self.ts("dve", ident[:], dfi, 0.0, None, ALU.is_equal, None, ["dfi"], ["ident"])
        S.op("pool", lambda e: e.memset(ones[:], 1.0), (), ["ones"])
        self.ts("dve", mask[:, 0, :], dfi, -64.0, None, ALU.is_le, None, ["dfi"], ["mask0"])
        dsq = self.carve([128, 128], F32)
        self.tt("dve", dsq, dfi, dfi, ALU.mult, ["dfi"], ["dsq"])
        self.ts("dve", mask[:, 1, :], dsq, 4096.0, None, ALU.is_le, None, ["dsq"], ["mask1"])
        self.ts("dve", mask[:, 2, :], dfi, 64.0, None, ALU.is_ge, None, ["dfi"], ["mask2"])
        invf = self.carve([128, 16], F32)
        vals = (np.float32(500000.0) ** (-(np.arange(0, 32, 2, dtype=np.float32)) / np.float32(32))).astype(np.float32)
        for i in range(16):
            v = float(vals[i])
            S.op("pool", (lambda v, i: (lambda e: e.memset(invf[:, i:i + 1], v)))(v, i), (), ["invf"])
        pos = self.carve([128, 3, NT], F32)
        S.op("pool", lambda e: e.iota(pos[:, 0, :], pattern=[[128, 16]], base=0, channel_multiplier=1,
                                      allow_small_or_imprecise_dtypes=True), (), ["pos0"])
        S.op("pool", lambda e: e.iota(pos[:, 1, :].rearrange("p (r j) -> p r j", j=4), pattern=[[1, 4], [512, 4]], base=0,
                                      channel_multiplier=4, allow_small_or_imprecise_dtypes=True), (), ["pos1"])
        S.op("pool", lambda e: e.iota(pos[:, 2, :], pattern=[[1, 16]], base=0, channel_multiplier=16,
                                      allow_small_or_imprecise_dtypes=True), (), ["pos2"])
        ang = self.carve([128, 3, NT, 16], F32)
        tmp = self.carve([128, 3, NT, 16], F32)
        self.tt("dve", ang, pos.unsqueeze(3).to_broadcast([128, 3, NT, 16]),
                invf.unsqueeze(1).unsqueeze(1).to_broadcast([128, 3, NT, 16]), ALU.mult,
                ["pos0", "pos1", "pos2", "invf"], ["ang"])
        cs = self.carve([128, 3, 2, NT, 16], F32)
        epst = self.eps_t
        S.op("pool", lambda e: e.memset(epst[:], LN_EPS), (), ["eps_t"])
        I32 = mybir.dt.int32
        for ci, off in ((0, 0.25), (1, 0.0)):
            y = self.carve([128, 3, NT, 16], F32)
            ki = self.carve([128, 3, NT, 16], I32)
            kf = self.carve([128, 3, NT, 16], F32)
            self.ts("dve", y, ang, 1.0 / (2.0 * math.pi), off, ALU.mult, ALU.add, ["ang"], ["ry%d" % ci])
            self.cp("dve", ki, y, ["ry%d" % ci], ["rk%d" % ci])
            self.cp("dve", kf, ki, ["rk%d" % ci], ["rkf%d" % ci])
            self.tt("dve", y, y, kf, ALU.subtract, ["ry%d" % ci, "rkf%d" % ci], ["ry%d" % ci])
            self.act(cs[:, :, ci, :, :], y, AF.Sin, ["ry%d" % ci], ["cs"], scale=2.0 * math.pi)
        self.dma(self.cs_d, cs.rearrange("p a b c d -> p (a b c d)"), ["cs"], (), key="cs_st")

    def load_x(self, x_d):
        for t in range(NT):
            self.dma(self.X[:, t, :], x_d[t * 128:(t + 1) * 128, :], (), ["X%d" % t])

    def store_x(self, y_d):
        for t in range(NT):
            self.dma(y_d[t * 128:(t + 1) * 128, :], self.X[:, t, :], ["X%d" % t], (), key="X%d" % t)

    def make_xt_tile(self, XT, t, xb, ps_tr, slot):
        xbs = xb[slot]
        self.cp("act", xbs, self.X[:, t, :], ["X%d" % t], ["xb%d" % slot])
        pv = ps_tr[slot]
        for kc in range(8):
            self.tr(pv[:, kc, :], xbs[:, kc * 128:(kc + 1) * 128], ["xb%d" % slot], ["pstr%d" % slot])
        self.cp("dve", XT[:, :, t * 128:(t + 1) * 128], pv, ["pstr%d" % slot], ["XT%d" % t])

    def make_xt(self, XT):
        xb = [self.carve([128, D], BF16) for _ in range(2)]
        ps_tr = [self.ps[i][:].bitcast(BF16).rearrange("p (a b) -> p a b", b=128) for i in (0, 1)]
        for t in range(NT):
            self.make_xt_tile(XT, t, xb, ps_tr, t % 2)

    def load_ln(self, g_d, b_d):
        G = self.carve([128, D], F32)
        B = self.carve([128, D], F32)
        self.dma(G, g_d.partition_broadcast(128), (), ["lnG"])
        gname = "lnG"
        self.dma(B, b_d.partition_broadcast(128), (), ["lnB"])
        bname = "lnB"
        st = self.carve([128, 2, 6], F32)
        mv = self.carve([128, 2], F32)
        rs = self.carve([128, 1], F32)
        return dict(G=G, B=B, gn=gname, bn=bname, st=st, mv=mv, rs=rs)

    def ln_tile(self, ln, src, srcn, dst, dstn):
        st, mv, rs = ln["st"], ln["mv"], ln["rs"]
        S = self.S
        for hf in range(2):
            S.op("dve", (lambda hf: (lambda e: e.bn_stats(out=st[:, hf, :], in_=src[:, hf * 512:(hf + 1) * 512])))(hf),
                 [srcn], ["ln_st"])
        S.op("dve", lambda e: e.bn_aggr(out=mv, in_=st.rearrange("p a b -> p (a b)")), ["ln_st"], ["ln_mv"])
        self.act(rs, mv[:, 1:2], AF.Ln, ["ln_mv"], ["ln_rs"], bias=self.eps_t[:, 0:1])
        self.act(rs, rs, AF.Exp, ["ln_rs"], ["ln_rs"], scale=-0.5)
        self.ts("dve", dst, src, mv[:, 0:1], rs, ALU.subtract, ALU.mult, [srcn, "ln_mv", "ln_rs"], [dstn])
        self.tt("pool", dst, dst, ln["G"], ALU.mult, [dstn, ln["gn"]], [dstn])
        self.tt("pool", dst, dst, ln["B"], ALU.add, [dstn, ln["bn"]], [dstn])

    def outproj_acc(self, ZT, ztn, wo_d_rows, first, wst, wbf, slot, ps_y):
        self.dma(wst, wo_d_rows, (), ["wost"])
        self.cp("pool", wbf[slot], wst, ["wost"], ["wobf%d" % slot])
        for t in range(NT):
            for hf in range(2):
                self.mm(ps_y[:, hf * 512:(hf + 1) * 512], ZT[:, t * 128:(t + 1) * 128],
                        wbf[slot][:, hf * 512:(hf + 1) * 512], True, True, [ztn, "wobf%d" % slot], ["psy"])
            xt = self.X[:, t, :]
            if first:
                self.stt("dve", xt, xt, ALPHA, ps_y, ALU.mult, ALU.add, ["X%d" % t, "psy"], ["X%d" % t])
            else:
                self.tt("dve", xt, xt, ps_y, ALU.add, ["X%d" % t, "psy"], ["X%d" % t])

    def attn_phase(self, XT, wqkv_d, wout_d, lng_d, lnb_d):
        nc = self.nc
        wst = self.carve([128, 8, 3, 128], F32)
        wgh = [self.carve([128, 8, 384], BF16) for _ in range(2)]
        qkv = self.carve([128, NT, 384], BF16)
        qkrot = self.carve([128, NT, 2, 32], F32)
        T = [self.carve([128, NT, 2, 16], F32) for _ in range(4)]
        QKT = self.carve([128, 2, S_LEN], BF16)
        OTa = self.carve([128, S_LEN], F32)
        DNa = self.carve([128, S_LEN], F32)
        OTn = self.carve([128, S_LEN], BF16)
        wost = self.carve([128, D], F32)
        wobf = [self.carve([128, D], BF16) for _ in range(2)]
        PT = [self.carve([128, 128], BF16) for _ in range(4)]
        ln = self.load_ln(lng_d, lnb_d)
        ps_proj = [self.ps[0], self.ps[1]]
        ps_trs = [self.ps[2 + i][:].bitcast(BF16)[:, 0:512].rearrange("p (t q d) -> p t q d", t=2, q=2) for i in range(2)]
        ps_Ss = [self.ps[0], self.ps[1]]
        ps_O = self.ps[4][:].rearrange("p (s n) -> p s n", n=128)
        ps_D = self.ps[5][:].rearrange("p (s n) -> p s n", n=128)
        ps_y = self.ps_y
        wv = wqkv_d.rearrange("(kc p) (g part h d) -> p kc g part h d", p=128, g=3, part=3, h=8)
        scale = 1.0 / math.sqrt(128.0)
        cs = self.carve([128, 3, 2, NT, 16], F32)
        self.dma(cs.rearrange("p a b c d -> p (a b c d)"), self.cs_d, (), ["cs"])
        pt_i = 0
        it = 0
        import os
        dbg = int(os.environ.get("ATT_DBG", "0"))
        for h in range(8 if not dbg else 1):
            for g in range(3):
                if dbg and g > 0 and dbg < 6:
                    continue
                if dbg and g > 1 and dbg < 7:
                    continue
                dil = DILS[g]
                L = S_LEN // dil
                nt = L // 128
                ws = it % 2
                it += 1
                for part in range(3):
                    self.dma(wst[:, :, part, :], wv[:, :, g, part, h, :], (), ["wst"], key="wst%d" % part)
                self.cp("pool", wgh[ws].rearrange("p k (a d) -> p k a d", a=3), wst, ["wst"], ["wgh%d" % ws])
                for tt in range(NT):
                    if g == 0:
                        c0, step = tt * 128, 1
                    else:
                        r, j = tt // nt, tt % nt
                        c0, step = r + dil * 128 * j, dil
                    pp = ps_proj[tt % 2]
                    for kc in range(8):
                        lhsT = XT[:, kc, c0:c0 + 127 * step + 1:step] if step > 1 else XT[:, kc, c0:c0 + 128]
                        self.mm(pp[:, 0:384], lhsT, wgh[ws][:, kc, :], kc == 0, kc == 7,
                                ["XT", "wgh%d" % ws], ["bank%d" % (tt % 2)])
                    self.cp("act", qkv[:, tt, :], pp[:, 0:384], ["bank%d" % (tt % 2)], ["qkv"])
                    self.cp("act", qkrot[:, tt, :, :], pp[:, 0:256].rearrange("p (a d) -> p a d", a=2)[:, :, 0:32],
                            ["bank%d" % (tt % 2)], ["qkrot"])
                if dbg == 1:
                    continue
                x1 = qkrot[:, :, :, 0:16]
                x2 = qkrot[:, :, :, 16:32]
                cb = cs[:, g, 0, :, :].unsqueeze(2).to_broadcast([128, NT, 2, 16])
                sb = cs[:, g, 1, :, :].unsqueeze(2).to_broadcast([128, NT, 2, 16])
                self.tt("dve", T[0], x1, cb, ALU.mult, ["qkrot", "cs"], ["T0"])
                self.tt("pool", T[1], x2, sb, ALU.mult, ["qkrot", "cs"], ["T1"])
                self.tt("dve", T[2], x2, cb, ALU.mult, ["qkrot", "cs"], ["T2"])
                self.tt("pool", T[3], x1, sb, ALU.mult, ["qkrot", "cs"], ["T3"])
                qk4 = qkv.rearrange("p t (a d) -> p t a d", a=3)
                self.tt("dve", qk4[:, :, 0:2, 0:16], T[0], T[1], ALU.subtract, ["T0", "T1", "qkv"], ["qkv"])
                self.tt("dve", qk4[:, :, 0:2, 16:32], T[2], T[3], ALU.add, ["T2", "T3", "qkv"], ["qkv"])
                if dbg == 2:
                    continue
                for b in range(NT // 2):
                    hs = b % 2
                    for t2 in range(2):
                        tt = b * 2 + t2
                        for a in range(2):
                            self.tr(ps_trs[hs][:, t2, a, :], qkv[:, tt, a * 128:(a + 1) * 128], ["qkv"], ["bank%d" % (2 + hs)])
                    if os.environ.get("NO_EVAC"):
                        continue
                    self.cp(os.environ.get("TR_ENG", "act"), QKT[:, :, b * 256:(b + 1) * 256].rearrange("p a (t d) -> p a t d", t=2),
                            ps_trs[hs].rearrange("p t a d -> p a t d"), ["bank%d" % (2 + hs)], ["QKT"])
                if dbg == 3:
                    continue
                for tq in range(NT):
                    r, jq = tq // nt, tq % nt
                    b4 = tq % 4
                    jks = [jk for jk in (jq - 1, jq, jq + 1) if 0 <= jk < nt]
                    qc = r * L + 128 * jq
                    for ii, jk in enumerate(jks):
                        kc0 = r * L + 128 * jk
                        s = pt_i % 4
                        pt_i += 1
                        self.mm(ps_Ss[s % 2][:, 0:128], QKT[:, 1, kc0:kc0 + 128], QKT[:, 0, qc:qc + 128], True, True,
                                ["QKT"], ["bank%d" % (s % 2)])
                        self.act(PT[s], ps_Ss[s % 2][:, 0:128], AF.Exp, ["bank%d" % (s % 2)], ["PT%d" % s], scale=scale)
                        self.tt(os.environ.get("MASK_ENG", "dve"), PT[s], PT[s], self.mask[:, jk - jq + 1, :], ALU.mult,
                                ["PT%d" % s], ["PT%d" % s])
                        tk = r * nt + jk
                        self.mm(ps_O[:, b4, :], qkv[:, tk, 256:384], PT[s], ii == 0, ii == len(jks) - 1,
                                ["qkv", "PT%d" % s], ["psO"])
                        self.mm(ps_D[:, b4, :], self.ones[:], PT[s], ii == 0, ii == len(jks) - 1,
                                ["PT%d" % s], ["psD"])
                    if b4 == 3:
                        tq0 = tq - 3
                        if g == 0:
                            oa = OTa[:, tq0 * 128:tq0 * 128 + 512].rearrange("p (s n) -> p s n", n=128)
                            da = DNa[:, tq0 * 128:tq0 * 128 + 512].rearrange("p (s n) -> p s n", n=128)
                        elif g == 1:
                            rr = tq0 // nt
                            oa = OTa[:, rr:S_LEN:4].rearrange("p (s n) -> p s n", n=128)
                            da = DNa[:, rr:S_LEN:4].rearrange("p (s n) -> p s n", n=128)
                        else:
                            r0 = tq0
                            oa = OTa.rearrange("p (m r) -> p r m", r=16)[:, r0:r0 + 4, :]
                            da = DNa.rearrange("p (m r) -> p r m", r=16)[:, r0:r0 + 4, :]
                        if g == 0:
                            self.cp("act", oa, ps_O, ["psO"], ["OTa"])
                            self.cp("act", da, ps_D, ["psD"], ["DNa"])
                        else:
                            self.tt("dve", oa, ps_O, oa, ALU.add, ["psO", "OTa"], ["OTa"])
                            self.tt("dve", da, ps_D, da, ALU.add, ["psD", "DNa"], ["DNa"])
            if dbg and dbg < 5:
                continue
            self.S.op("dve", lambda e: e.reciprocal(out=DNa, in_=DNa), ["DNa"], ["DNa"])
            self.tt("dve", OTn, OTa, DNa, ALU.mult, ["OTa", "DNa"], ["OTn"])
            self.outproj_acc(OTn, "OTn", wout_d[h * 128:(h + 1) * 128, :], h == 0, wost, wobf, h % 2, ps_y)
        for t in range(NT):
            self.ln_tile(ln, self.X[:, t, :], "X%d" % t, self.X[:, t, :], "X%d" % t)

    def conv_phase(self, XT, win_d, ck_d, wout_d, lng_d, lnb_d):
        wst = self.carve([128, 8, 3, 128], F32)
        wgh = [self.carve([128, 8, 3, 128], BF16) for _ in range(2)]
        HT = self.carve([128, S_LEN + 2], F32)
        GB = self.carve([128, S_LEN], F32)
        CV = self.carve([128, S_LEN], F32)
        ZT = self.carve([128, S_LEN], BF16)
        gct = [self.carve([128, 512], F32) for _ in range(2)]
        ck = self.carve([128, 3, 8], F32)
        wost = self.carve([128, D], F32)
        wobf = [self.carve([128, D], BF16) for _ in range(2)]
        ln = self.load_ln(lng_d, lnb_d)
        ps_y = self.ps_y
        for w_ in range(3):
            self.S.dma("sp", (lambda w_: (lambda e: e.dma_start(out=ck[:, w_, :], in_=ck_d[w_].rearrange("(cb p) -> p cb", p=128),
                                                                allow_slow_non_contiguous=True)))(w_), (), ["ck"], key="ck%d" % w_)
        HTn = HT
        self.S.op("pool", lambda e: e.memset(HTn[:, 0:1], 0.0), (), ["HTpad0"])
        self.S.op("pool", lambda e: e.memset(HTn[:, S_LEN + 1:S_LEN + 2], 0.0), (), ["HTpad1"])
        wv = win_d.rearrange("(kc p) (part cb d) -> p kc part cb d", p=128, part=3, cb=8)
        for cb in range(8):
            ws = cb % 2
            for part in range(3):
                self.dma(wst[:, :, part, :], wv[:, :, part, cb, :], (), ["wst"], key="wst%d" % part)
            self.cp("pool", wgh[ws], wst, ["wst"], ["wgh%d" % ws])
            for tb in range(4):
                pps = [self.ps[0 + (tb % 2) * 3 + part] for part in range(3)]
                for part in range(3):
                    for kc in range(8):
                        self.mm(pps[part][:, :], wgh[ws][:, kc, part, :], XT[:, kc, tb * 512:(tb + 1) * 512],
                                kc == 0, kc == 7, ["XT", "wgh%d" % ws], ["pc%d_%d" % (tb % 2, part)])
                gs = tb % 2
                self.cp("act", gct[gs], pps[1][:, :], ["pc%d_1" % (tb % 2)], ["gct%d" % gs])
                self.tt("dve", HT[:, 1 + tb * 512:1 + (tb + 1) * 512], gct[gs], pps[2][:, :], ALU.mult,
                        ["gct%d" % gs, "pc%d_2" % (tb % 2)], ["HT"])
                self.cp("act", GB[:, tb * 512:(tb + 1) * 512], pps[0][:, :], ["pc%d_0" % (tb % 2)], ["GB"])
            self.ts("pool", CV, HT[:, 1:S_LEN + 1], ck[:, 1, cb:cb + 1], None, ALU.mult, None, ["HT", "ck", "HTpad0", "HTpad1"], ["CV"])
            self.stt("dve", CV, HT[:, 0:S_LEN], ck[:, 0, cb:cb + 1], CV, ALU.mult, ALU.add, ["HT", "ck", "CV", "HTpad0"], ["CV"])
            self.stt("dve", CV, HT[:, 2:S_LEN + 2], ck[:, 2, cb:cb + 1], CV, ALU.mult, ALU.add, ["HT", "ck", "CV", "HTpad1"], ["CV"])
            self.tt("pool", ZT, GB, CV, ALU.mult, ["GB", "CV"], ["ZT"])
            self.outproj_acc(ZT, "ZT", wout_d[cb * 128:(cb + 1) * 128, :], cb == 0, wost, wobf, cb % 2, ps_y)
        for t in range(NT):
            self.ln_tile(ln, self.X[:, t, :], "X%d" % t, self.X[:, t, :], "X%d" % t)


    def prep_tables(self, u_d, v_d, UTs, Vs, layers):
        us = [self.carve([128, D], F32) for _ in range(2)]
        vs = [self.carve([128, D], F32) for _ in range(2)]
        ub = [self.carve([128, D], BF16) for _ in range(2)]
        utb = [self.carve([128, D], BF16) for _ in range(2)]
        vbb = [self.carve([128, D], BF16) for _ in range(2)]
        it = 0
        for l in layers:
            for b in range(128):
                sl = it % 2
                it += 1
                row0 = (l * 128 + b) * 128
                self.dma(us[sl], u_d[l, b * 128:(b + 1) * 128, :], (), ["us%d" % sl])
                self.dma(vs[sl], v_d[l, b * 128:(b + 1) * 128, :], (), ["vs%d" % sl])
                self.cp("act", ub[sl], us[sl], ["us%d" % sl], ["ub%d" % sl])
                pv = self.ps[sl][:].bitcast(BF16)
                for kc in range(8):
                    self.tr(pv[:, kc * 128:(kc + 1) * 128], ub[sl][:, kc * 128:(kc + 1) * 128], ["ub%d" % sl], ["bank%d" % sl])
                self.cp("dve", utb[sl], pv, ["bank%d" % sl], ["utb%d" % sl])
                self.dma(UTs[row0:row0 + 128, :], utb[sl], ["utb%d" % sl], (), key="utb%d" % sl)
                self.cp("pool", vbb[sl], vs[sl], ["vs%d" % sl], ["vbb%d" % sl])
                self.dma(Vs[row0:row0 + 128, :], vbb[sl], ["vbb%d" % sl], (), key="vbb%d" % sl)

    def topk16(self, srcs, vals, idxs, tmps, rn, wn):
        S = self.S
        n = len(srcs)

        def mk(f):
            return f
        for i in range(n):
            S.op("dve", (lambda v, s_: (lambda e: e.max(out=v[:, 0:8], in_=s_)))(vals[i], srcs[i]), rn[i], [wn[i] + "v0"])
        for i in range(n):
            S.op("dve", (lambda x, v, s_: (lambda e: e.max_index(out=x[:, 0:8], in_max=v[:, 0:8], in_values=s_)))(idxs[i], vals[i], srcs[i]),
                 rn[i] + [wn[i] + "v0"], [wn[i] + "i0"])
            S.op("dve", (lambda t, v, s_: (lambda e: e.match_replace(out=t, in_to_replace=v[:, 0:8], in_values=s_, imm_value=NEG)))(tmps[i], vals[i], srcs[i]),
                 rn[i] + [wn[i] + "v0"], [wn[i] + "t"])
        for i in range(n):
            S.op("dve", (lambda v, t: (lambda e: e.max(out=v[:, 8:16], in_=t)))(vals[i], tmps[i]), [wn[i] + "t"], [wn[i] + "v1"])
        for i in range(n):
            S.op("dve", (lambda x, v, t: (lambda e: e.max_index(out=x[:, 8:16], in_max=v[:, 8:16], in_values=t)))(idxs[i], vals[i], tmps[i]),
                 [wn[i] + "t", wn[i] + "v1"], [wn[i] + "i1"])

    def peer_phase(self, l, wq_d, sk_d, UTs, Vs, lng_d, lnb_d, tiles):
        S = self.S
        I32 = mybir.dt.int32
        Wq = self.carve([128, 8, 2048], BF16)
        R1 = self.carve([128, 2048], F32)
        R2 = self.carve([128, 2048], F32)
        skT = self.carve([128, 16, 128], BF16)
        xb = self.carve([128, D], BF16)
        xTc = self.carve([128, 8, 128], BF16)

        SV = self.carve([128, 16, 16], F32)
        SI = self.carve([128, 16, 16], U32)
        SIf = self.carve([128, 16, 16], F32)
        ctmp = [self.carve([128, 256], F32) for _ in range(2)]
        tmpA = [c[:, 0:128] for c in ctmp]
        FV = self.carve([128, 8, 16], F32)
        FI = self.carve([128, 8, 16], U32)
        FIf = self.carve([128, 8, 16], F32)
        Tq = self.carve([128, 8, 16], F32)
        Ti = self.carve([128, 8, 16], I32)
        If_ = self.carve([128, 8, 16], F32)
        Jf_ = self.carve([128, 8, 16], F32)
        Ee = self.carve([128, 8, 16], F32)
        Zz = self.carve([128, 8], F32)
        GI = self.carve([128, 3, 128], BF16)
        GIT = self.carve([128, 3, 128], F32)
        NPC = 8
        qT = self.carve([128, 16, 128], BF16)
        Ab = [qT[:, 0:8, :], qT[:, 8:16, :]]
        Bb = [self.carve([128, NPC, 128], BF16) for _ in range(2)]
        Wt = self.carve([128, 128, 128], BF16)
        skb = Wt[:, 0:16, :]
        NSL = 3
        ut = [self.carve([128, 8, 128], BF16) for _ in range(NSL)]
        vb = [self.carve([128, D], BF16) for _ in range(NSL)]
        Hs = [self.carve([128, 128], BF16) for _ in range(2)]
        Hw = [self.carve([128, 128], BF16) for _ in range(2)]
        ln = self.load_ln(lng_d, lnb_d)
        cand = R1.rearrange("p (h i j) -> p h i j", h=8, i=16)
        SC = R2.rearrange("p (b n) -> p b n", n=128)
        OH = R2.rearrange("p (h k i) -> p h k i", h=8, k=16)
        iota_n = self.iota_n
        ps = self.ps
        ps_y = self.ps_y
        wqv = wq_d.rearrange("(kc p) c -> p kc c", p=128)
        st = R1.rearrange("p (k c) -> p k c", k=8)
        for c8 in range(8):
            self.dma(st, wqv[:, :, c8 * 256:(c8 + 1) * 256], (), ["R1"])
            self.cp("pool", Wq[:, :, c8 * 256:(c8 + 1) * 256], st, ["R1"], ["Wq"])
        sks = R1.rearrange("p (b k) -> p b k", k=128)
        self.dma(sks, sk_d.rearrange("h p n k -> n (h p) k"), (), ["R1"])
        self.cp("pool", skb, sks, ["R1"], ["skb", "Wt"])
        for grp in range(4):
            pv = ps[grp % 2][:].bitcast(BF16)[:, 0:512].rearrange("p (b n) -> p b n", n=128)
            for j in range(4):
                self.tr(pv[:, j, :], skb[:, grp * 4 + j, :], ["skb", "Wt"], ["bank%d" % (grp % 2)])
            self.cp("dve", skT[:, grp * 4:(grp + 1) * 4, :], pv, ["bank%d" % (grp % 2)], ["skT"])
        scr_u = UTs.rearrange("(r p) (k e) -> r p k e", p=128, k=8)
        scr_v = Vs.rearrange("(r p) d -> r p d", p=128)
        for t in tiles:
            xn = "X%d" % t
            self.cp("act", xb, self.X[:, t, :], [xn], ["pxb"])
            pv = ps[0][:].bitcast(BF16).rearrange("p (k n) -> p k n", n=128)
            for kc in range(8):
                self.tr(pv[:, kc, :], xb[:, kc * 128:(kc + 1) * 128], ["pxb"], ["bank0"])
            self.cp("dve", xTc, pv, ["bank0"], ["xTc"])
            for grp in range(4):
                bk = (1, 4)[grp % 2]
                for j in range(4):
                    blk = grp * 4 + j
                    for kc in range(8):
                        self.mm(ps[bk][:, j * 128:(j + 1) * 128], Wq[:, kc, blk * 128:(blk + 1) * 128], xTc[:, kc, :],
                                kc == 0, kc == 7, ["Wq", "xTc"], ["bank%d" % bk])
                self.cp("act", qT[:, grp * 4:(grp + 1) * 4, :], ps[bk][:].rearrange("p (b n) -> p b n", n=128),
                        ["bank%d" % bk], ["qT", "Ab0", "Ab1"])
            for qd in range(4):
                bk = 2 + qd % 2
                for j in range(4):
                    blk = qd * 4 + j
                    self.mm(ps[bk][:, j * 128:(j + 1) * 128], qT[:, blk, :], skT[:, blk, :], True, True,
                            ["qT", "skT"], ["bank%d" % bk])
                self.cp("act", SC[:, qd * 4:(qd + 1) * 4, :], ps[bk][:].rearrange("p (b n) -> p b n", n=128),
                        ["bank%d" % bk], ["SC%d" % qd, "R2"])
            for b2 in range(8):
                blks = (2 * b2, 2 * b2 + 1)
                self.topk16([SC[:, b, :] for b in blks], [SV[:, b, :] for b in blks], [SI[:, b, :] for b in blks],
                            tmpA, [["SC%d" % (b // 4)] for b in blks], ["l1_%d" % (b % 2) for b in blks])
            l1v = ["l1_0v0", "l1_0v1", "l1_1v0", "l1_1v1"]
            l1i = ["l1_0i0", "l1_0i1", "l1_1i0", "l1_1i1"]
            self.cp("dve", SIf, SI, l1i, ["SIf"])
            SV4 = SV.rearrange("p (h two) k -> p h two k", two=2)
            self.tt("pool", cand, SV4[:, :, 0, :].unsqueeze(3).to_broadcast([128, 8, 16, 16]),
                    SV4[:, :, 1, :].unsqueeze(2).to_broadcast([128, 8, 16, 16]), ALU.add, l1v, ["cand", "R1"])
            for h2 in range(4):
                hs_ = (2 * h2, 2 * h2 + 1)
                self.topk16([cand[:, h].rearrange("p i j -> p (i j)") for h in hs_], [FV[:, h, :] for h in hs_],
                            [FI[:, h, :] for h in hs_], ctmp, [["cand"] for _ in hs_], ["l2_%d" % (h % 2) for h in hs_])
            l2v = ["l2_0v0", "l2_0v1", "l2_1v0", "l2_1v1"]
            l2i = ["l2_0i0", "l2_0i1", "l2_1i0", "l2_1i1"]
            self.cp("dve", FIf, FI, l2i, ["FIf"])
            self.ts("dve", Tq, FIf, 1.0 / 16.0, -7.5 / 16.0, ALU.mult, ALU.add, ["FIf"], ["Tq"])
            self.cp("dve", Ti, Tq, ["Tq"], ["Ti"])
            self.cp("dve", If_, Ti, ["Ti"], ["If"])
            self.stt("dve", Jf_, If_, -16.0, FIf, ALU.mult, ALU.add, ["If", "FIf"], ["Jf"])
            SI4 = SIf.rearrange("p (h two) k -> p h two k", two=2)
            io16 = iota_n[:, 0:16].unsqueeze(1).unsqueeze(1).to_broadcast([128, 8, 16, 16])
            for which, src, sn in ((0, If_, "If"), (1, Jf_, "Jf")):
                self.tt("dve", OH, src.unsqueeze(3).to_broadcast([128, 8, 16, 16]), io16, ALU.is_equal,
                        [sn, "iota_n", "R2"], ["OH", "R2", "SC0", "SC1", "SC2", "SC3"])
                self.tt("dve", OH, OH, SI4[:, :, which, :].unsqueeze(2).to_broadcast([128, 8, 16, 16]), ALU.mult,
                        ["OH", "SIf"], ["OH", "R2"])
                dst = Tq if which == 0 else Ee
                S.op("dve", (lambda dst: (lambda e: e.tensor_reduce(out=dst, in_=OH, axis=AX.X, op=ALU.add)))(dst),
                     ["OH"], ["Tq" if which == 0 else "Ee"])
                self.cp("dve", GI[:, 1 + which, :].rearrange("p (h k) -> p h k", k=16), dst,
                        ["Tq" if which == 0 else "Ee"], ["GI%d" % (1 + which)])
            self.tt("dve", Ee, FV, FV[:, :, 0:1].to_broadcast([128, 8, 16]), ALU.subtract, l2v, ["Ee"])
            self.act(Ee, Ee, AF.Exp, ["Ee"], ["Ee"])
            S.op("dve", lambda e: e.tensor_reduce(out=Zz, in_=Ee, axis=AX.X, op=ALU.add), ["Ee"], ["Zz"])
            S.op("dve", lambda e: e.reciprocal(out=Zz, in_=Zz), ["Zz"], ["Zz"])
            self.tt("dve", GI[:, 0, :].rearrange("p (h k) -> p h k", k=16), Ee, Zz.unsqueeze(2).to_broadcast([128, 8, 16]),
                    ALU.mult, ["Ee", "Zz"], ["GI0"])
            pv = ps[0][:].bitcast(BF16)[:, 0:384].rearrange("p (a n) -> p a n", n=128)
            for a in range(3):
                self.tr(pv[:, a, :], GI[:, a, :], ["GI%d" % a], ["bank0"])
            self.cp("dve", GIT, pv, ["bank0"], ["GIT"])
            for pc in range(128 // NPC):
                sl = pc % 2
                c0 = pc * NPC
                ion = iota_n[:].unsqueeze(1).to_broadcast([128, NPC, 128])
                self.tt("dve", Ab[sl], ion, GIT[:, 1, c0:c0 + NPC].unsqueeze(2).to_broadcast([128, NPC, 128]), ALU.is_equal,
                        ["GIT", "iota_n"], ["Ab%d" % sl, "qT"])
                self.tt("dve", Ab[sl], Ab[sl], GIT[:, 0, c0:c0 + NPC].unsqueeze(2).to_broadcast([128, NPC, 128]), ALU.mult,
                        ["GIT", "Ab%d" % sl], ["Ab%d" % sl])
                self.tt("dve", Bb[sl], ion, GIT[:, 2, c0:c0 + NPC].unsqueeze(2).to_broadcast([128, NPC, 128]), ALU.is_equal,
                        ["GIT", "iota_n"], ["Bb%d" % sl])
                for q4 in range(NPC // 4):
                    bk = 4 + ((pc * (NPC // 4) + q4) % 2)
                    for cc in range(4):
                        ci = q4 * 4 + cc
                        self.mm(ps[bk][:, cc * 128:(cc + 1) * 128], Bb[sl][:, ci, :], Ab[sl][:, ci, :], True, True,
                                ["Ab%d" % sl, "Bb%d" % sl], ["bank%d" % bk])
                    cs0 = c0 + q4 * 4
                    self.cp("act" if (q4 % 2 == 0) else "dve", Wt[:, cs0:cs0 + 4, :],
                            ps[bk][:].rearrange("p (c n) -> p c n", n=128), ["bank%d" % bk], ["Wt"])
            for b in range(128):
                sl = b % NSL
                r = l * 128 + b
                self.dma(ut[sl], scr_u[r], (), ["ut%d" % sl])
                self.dma(vb[sl], scr_v[r], (), ["vb%d" % sl])
                bk = b % 4
                for kc in range(8):
                    self.mm(ps[bk][:, 0:128], ut[sl][:, kc, :], xTc[:, kc, :], kc == 0, kc == 7,
                            ["ut%d" % sl, "xTc"], ["bank%d" % bk])
                h2 = b % 2
                self.act(Hs[h2], ps[bk][:, 0:128], AF.Gelu, ["bank%d" % bk], ["Hs%d" % h2])
                self.tt("dve", Hw[h2], Hs[h2], Wt[:, :, b], ALU.mult, ["Hs%d" % h2, "Wt"], ["Hw%d" % h2])
                for hf in range(2):
                    self.mm(ps_y[:, hf * 512:(hf + 1) * 512], Hw[h2], vb[sl][:, hf * 512:(hf + 1) * 512], b == 0, b == 127,
                            ["Hw%d" % h2, "vb%d" % sl], ["psy"])
            xt_ = self.X[:, t, :]
            self.stt("dve", xt_, xt_, ALPHA, ps_y, ALU.mult, ALU.add, [xn, "psy"], [xn])
            self.ln_tile(ln, xt_, xn, xt_, xn)

    def ple_phase(self, XT, wg_d, wp_d, p_d, tiles):
        Wg = self.carve([128, 8, D], BF16)
        Wp = self.carve([128, 2, D], BF16)
        stg = self.carve([128, 8, 256], F32)
        xb = [self.carve([128, D], BF16) for _ in range(2)]
        xTc = self.carve([128, 8, 128], BF16)
        pt = self.carve([128, 256], F32)
        ptb = self.carve([128, 256], BF16)
        pT = self.carve([128, 2, 128], BF16)
        sig = self.carve([128, D], F32)
        ps = self.ps
        wgv = wg_d.rearrange("(kc p) c -> p kc c", p=128)
        for c4 in range(4):
            self.dma(stg, wgv[:, :, c4 * 256:(c4 + 1) * 256], (), ["stg"])
            self.cp("pool", Wg[:, :, c4 * 256:(c4 + 1) * 256], stg, ["stg"], ["Wg"])
        wpv = wp_d.rearrange("(kc p) c -> p kc c", p=128)
        stp = stg.rearrange("p k c -> p (k c)")[:, 0:2048].rearrange("p (k c) -> p k c", k=2)
        self.dma(stp, wpv, (), ["stg"])
        self.cp("pool", Wp, stp, ["stg"], ["Wp"])
        for t in tiles:
            xn = "X%d" % t
            self.cp("act", xb[0], self.X[:, t, :], [xn], ["xb0"])
            pv = ps[0][:].bitcast(BF16).rearrange("p (k n) -> p k n", n=128)
            for kc in range(8):
                self.tr(pv[:, kc, :], xb[0][:, kc * 128:(kc + 1) * 128], ["xb0"], ["bank0"])
            self.cp("dve", xTc, pv, ["bank0"], ["xTc"])
            self.dma(pt, p_d[t * 128:(t + 1) * 128, :], (), ["pt"])
            self.cp("pool", ptb, pt, ["pt"], ["ptb"])
            pv2 = ps[2][:].bitcast(BF16)[:, 0:256].rearrange("p (k n) -> p k n", n=128)
            for kc in range(2):
                self.tr(pv2[:, kc, :], ptb[:, kc * 128:(kc + 1) * 128], ["ptb"], ["bank2"])
            self.cp("dve", pT, pv2, ["bank2"], ["pT"])
            for hf in range(2):
                for kc in range(8):
                    self.mm(ps[4 + hf][:, :], xTc[:, kc, :], Wg[:, kc, hf * 512:(hf + 1) * 512], kc == 0, kc == 7,
                            ["xTc", "Wg"], ["bank%d" % (4 + hf)])
            for hf in range(2):
                for kc in range(2):
                    self.mm(self.ps_y[:, hf * 512:(hf + 1) * 512], pT[:, kc, :], Wp[:, kc, hf * 512:(hf + 1) * 512], kc == 0, kc == 1,
                            ["pT", "Wp"], ["psy"])
            for hf in range(2):
                self.act(sig[:, hf * 512:(hf + 1) * 512], ps[4 + hf][:, :], AF.Sigmoid, ["bank%d" % (4 + hf)], ["sig"])
            self.tt("dve", sig, sig, self.ps_y, ALU.mult, ["sig", "psy"], ["sig"])
            self.tt("pool", self.X[:, t, :], self.X[:, t, :], sig, ALU.add, [xn, "sig"], [xn])
            self.cp("act", xb[1], self.X[:, t, :], [xn], ["xb1"])
            pv3 = ps[1][:].bitcast(BF16).rearrange("p (k n) -> p k n", n=128)
            for kc in range(8):
                self.tr(pv3[:, kc, :], xb[1][:, kc * 128:(kc + 1) * 128], ["xb1"], ["bank1"])
            self.cp("dve", XT[:, :, t * 128:(t + 1) * 128], pv3, ["bank1"], ["XT%d" % t])


def build_test_mixer(kind):
    nc = bass.Bass("TRN2", target_bir_lowering=False)
    x_d = nc.dram_tensor("x", [S_LEN, D], F32, kind="ExternalInput").ap()
    y_d = nc.dram_tensor("y", [S_LEN, D], F32, kind="ExternalOutput").ap()
    lng = nc.dram_tensor("lng", [D], F32, kind="ExternalInput").ap()
    lnb = nc.dram_tensor("lnb", [D], F32, kind="ExternalInput").ap()
    wout = nc.dram_tensor("wout", [D, D], F32, kind="ExternalInput").ap()
    if kind == "attn":
        w1 = nc.dram_tensor("wqkv", [D, 9216], F32, kind="ExternalInput").ap()
    elif kind == "conv":
        w1 = nc.dram_tensor("win", [D, 3072], F32, kind="ExternalInput").ap()
        ckd = nc.dram_tensor("ck", [3, D], F32, kind="ExternalInput").ap()
    kb = KB(nc)
    kb.build_consts()
    kb.barrier()
    kb.load_x(x_d)
    XT = kb.carve([128, 8, S_LEN], BF16)
    mark = kb.aoff
    kb.make_xt(XT)
    kb.barrier(keep=mark)
    if kind == "attn":
        kb.attn_phase(XT, w1, wout, lng, lnb)
    elif kind == "conv":
        kb.conv_phase(XT, w1, ckd, wout, lng, lnb)
    elif kind == "ln":
        ln = kb.load_ln(lng, lnb)
        for t in range(NT):
            kb.ln_tile(ln, kb.X[:, t, :], "X%d" % t, kb.X[:, t, :], "X%d" % t)
    kb.store_x(y_d)
    kb.S.emit_all()
    return nc


def build_test_peer(ntiles, do_ple=False):
    nc = bass.Bass("TRN2", target_bir_lowering=False)
    x_d = nc.dram_tensor("x", [S_LEN, D], F32, kind="ExternalInput").ap()
    y_d = nc.dram_tensor("y", [S_LEN, D], F32, kind="ExternalOutput").ap()
    lng = nc.dram_tensor("lng", [D], F32, kind="ExternalInput").ap()
    lnb = nc.dram_tensor("lnb", [D], F32, kind="ExternalInput").ap()
    wq = nc.dram_tensor("wq", [D, 2048], F32, kind="ExternalInput").ap()
    sk = nc.dram_tensor("sk", [8, 2, 128, 128], F32, kind="ExternalInput").ap()
    u_d = nc.dram_tensor("u", [1, 16384, D], F32, kind="ExternalInput").ap()
    v_d = nc.dram_tensor("v", [1, 16384, D], F32, kind="ExternalInput").ap()
    UTs = nc.dram_tensor("UTs", [128 * 128, D], BF16).ap()
    Vs = nc.dram_tensor("Vs", [128 * 128, D], BF16).ap()
    if do_ple:
        wg = nc.dram_tensor("wg", [D, D], F32, kind="ExternalInput").ap()
        wp = nc.dram_tensor("wp", [256, D], F32, kind="ExternalInput").ap()
        pp = nc.dram_tensor("pp", [S_LEN, 256], F32, kind="ExternalInput").ap()
    kb = KB(nc)
    kb.build_consts()
    kb.barrier()
    kb.prep_tables(u_d, v_d, UTs, Vs, [0])
    kb.load_x(x_d)
    kb.barrier()
    tiles = list(range(ntiles))
    kb.peer_phase(0, wq, sk, UTs, Vs, lng, lnb, tiles)
    if do_ple:
        kb.barrier()
        XT = kb.carve([128, 8, S_LEN], BF16)
        kb.ple_phase(XT, wg, wp, pp, tiles)
    kb.store_x(y_d)
    kb.S.emit_all()
    return nc


WEIGHT_SPECS = [
    ("attn_w_qkv", [2, D, 9216]), ("attn_w_out", [2, D, D]), ("conv_w_in", [2, D, 3072]),
    ("conv_kernel", [2, 3, D]), ("conv_w_out", [2, D, D]), ("peer_w_query", [DEPTH, D, 2048]),
    ("peer_sub_keys", [DEPTH, 8, 2, 128, 128]), ("peer_u", [DEPTH, 16384, D]), ("peer_v", [DEPTH, 16384, D]),
    ("ln1_gain", [DEPTH, D]), ("ln1_bias", [DEPTH, D]), ("ln2_gain", [DEPTH, D]), ("ln2_bias", [DEPTH, D]),
    ("ple_w_proj", [DEPTH, 256, D]), ("ple_w_gate", [DEPTH, D, D]),
]


def build_full(nseq=3, depth=DEPTH):
    nc = bass.Bass("TRN2", target_bir_lowering=False)
    xs = nc.dram_tensor("xs", [nseq, S_LEN, D], F32, kind="ExternalInput").ap()
    pps = nc.dram_tensor("pps", [DEPTH, nseq, S_LEN, 256], F32, kind="ExternalInput").ap()
    ys = nc.dram_tensor("ys", [nseq, S_LEN, D], F32, kind="ExternalOutput").ap()
    W = {n: nc.dram_tensor(n, sh, F32, kind="ExternalInput").ap() for n, sh in WEIGHT_SPECS}
    UTs = nc.dram_tensor("UTs", [DEPTH * 128 * 128, D], BF16).ap()
    Vs = nc.dram_tensor("Vs", [DEPTH * 128 * 128, D], BF16).ap()
    kb = KB(nc)
    kb.build_consts()
    kb.barrier()
    kb.prep_tables(W["peer_u"], W["peer_v"], UTs, Vs, list(range(depth)))
    kb.barrier()
    tiles = list(range(NT))
    XTB = 8 * S_LEN * 2
    for s in range(nseq):
        kb.load_x(xs[s])
        XT = kb.carve([128, 8, S_LEN], BF16)
        kb.make_xt(XT)
        kb.barrier(keep=XTB)
        for i in range(depth):
            j = i // 2
            if i % 2 == 0:
                kb.attn_phase(XT, W["attn_w_qkv"][j], W["attn_w_out"][j], W["ln1_gain"][i], W["ln1_bias"][i])
            else:
                kb.conv_phase(XT, W["conv_w_in"][j], W["conv_kernel"][j], W["conv_w_out"][j], W["ln1_gain"][i], W["ln1_bias"][i])
            kb.barrier()
            kb.peer_phase(i, W["peer_w_query"][i], W["peer_sub_keys"][i], UTs, Vs, W["ln2_gain"][i], W["ln2_bias"][i], tiles)
            kb.barrier()
            XT = kb.carve([128, 8, S_LEN], BF16)
            kb.ple_phase(XT, W["ple_w_gate"][i], W["ple_w_proj"][i], pps[i, s], tiles)
            kb.barrier(keep=XTB)
        kb.store_x(ys[s])
        kb.barrier()
    kb.S.emit_all()
    return nc


_NC_CACHE = {}


def kernel(x_prompt, x_sample, p_prompt, p_sample, attn_w_qkv, attn_w_out, conv_w_in, conv_kernel, conv_w_out,
           peer_w_query, peer_sub_keys, peer_u, peer_v, ln1_gain, ln1_bias, ln2_gain, ln2_bias, ple_w_proj, ple_w_gate):
    n = 8
    f32 = lambda a: np.ascontiguousarray(np.asarray(a, dtype=np.float32))
    x_prompt, x_sample, p_prompt, p_sample = f32(x_prompt), f32(x_sample), f32(p_prompt), f32(p_sample)
    wts = dict(attn_w_qkv=f32(attn_w_qkv), attn_w_out=f32(attn_w_out), conv_w_in=f32(conv_w_in), conv_kernel=f32(conv_kernel),
               conv_w_out=f32(conv_w_out), peer_w_query=f32(peer_w_query), peer_sub_keys=f32(peer_sub_keys),
               peer_u=f32(peer_u), peer_v=f32(peer_v), ln1_gain=f32(ln1_gain), ln1_bias=f32(ln1_bias),
               ln2_gain=f32(ln2_gain), ln2_bias=f32(ln2_bias), ple_w_proj=f32(ple_w_proj), ple_w_gate=f32(ple_w_gate))
    if "nc" not in _NC_CACHE:
        _NC_CACHE["nc"] = build_full()
    nc = _NC_CACHE["nc"]
    in_maps = []
    for c in range(n):
        xs = np.stack([x_prompt[c], x_sample[2 * c], x_sample[2 * c + 1]], axis=0)
        pp = np.stack([p_prompt[:, c], p_sample[:, 2 * c], p_sample[:, 2 * c + 1]], axis=1)
        m = {"xs": np.ascontiguousarray(xs), "pps": np.ascontiguousarray(pp)}
        m.update(wts)
        in_maps.append(m)
    res = run_bass_kernel_spmd(nc, in_maps, core_ids=list(range(n)))
    y_prompt = np.empty_like(x_prompt)
    y_sample = np.empty_like(x_sample)
    for c in range(n):
        ys = res.results[c]["ys"]
        y_prompt[c] = ys[0]
        y_sample[2 * c] = ys[1]
        y_sample[2 * c + 1] = ys[2]
    return (y_prompt, y_sample)
```
